# Optimizing a Trainium2 kernel written in Bass

```python
import jax, jax.numpy as jnp
from jax import lax
import numpy as np

D_MODEL = 1024
BATCH = 8
SEQ = 8192
DEPTH = 4

D_FF = 2816
NORM_EPS = 1e-5
CONV_W = 3
HEAD_SIZE = 64
N_HEADS = D_MODEL // HEAD_SIZE
D_DECAY_LORA = 64
D_AAA_LORA = 64
D_MV_LORA = 32
D_GATE_LORA = 160
GN_EPS = 64e-5
N_MIX = 6
N_CONV = (DEPTH + 1) // 2
N_RWKV = DEPTH // 2
N_VRES = max(N_RWKV - 1, 0)

kernel_name = "hybrid_shortconv_rwkv7_macaron"


def rms_norm(x, g):
    xf = x.astype(jnp.float32)
    y = xf * lax.rsqrt(jnp.mean(xf * xf, axis=-1, keepdims=True) + NORM_EPS)
    return (y * g.astype(jnp.float32)).astype(x.dtype)


def swiglu(x, w_gu, w_d):
    gate, up = jnp.split(x @ w_gu, 2, axis=-1)
    return (jax.nn.silu(gate) * up) @ w_d


def short_conv_mixer(x, w_in, conv_w, w_out):
    gb, gc, h = jnp.split(x @ w_in, 3, axis=-1)
    z = gc * h
    z = lax.conv_general_dilated(
        z, conv_w[:, None, :].astype(z.dtype), window_strides=(1,),
        padding=[(CONV_W - 1, 0)], dimension_numbers=('NWC', 'WIO', 'NWC'),
        feature_group_count=D_MODEL)
    return (gb * z) @ w_out


def wkv7_scan(r, w, k, v, a, b):
    tm = lambda t: jnp.moveaxis(t.astype(jnp.float32), 1, 0)

    def step(S, inp):
        r_t, w_t, k_t, v_t, a_t, b_t = inp
        sa = jnp.einsum('bhij,bhj->bhi', S, a_t)
        S = S * w_t[:, :, None, :] + sa[..., None] * b_t[:, :, None, :] + v_t[..., None] * k_t[:, :, None, :]
        y = jnp.einsum('bhij,bhj->bhi', S, r_t)
        return S, y

    B, _, H, N = r.shape
    S0 = jnp.zeros((B, H, N, N), jnp.float32)
    _, y = lax.scan(step, S0, (tm(r), tm(w), tm(k), tm(v), tm(a), tm(b)))
    return jnp.moveaxis(y, 0, 1)


def rwkv7_mixer(x, v_first, mu, w_rkv, w0, w1, w2, a0, a1, a2, g1, g2,
                k_k, k_a, r_k, ln_w, ln_b, w_o, v_res):
    B, S, D = x.shape
    H, N = N_HEADS, HEAD_SIZE
    xx = jnp.pad(x, ((0, 0), (1, 0), (0, 0)))[:, :-1] - x
    xs = x[None] + xx[None] * mu[:, None, None, :]
    r, k, v = jnp.einsum('pbsd,pde->pbse', xs[:3], w_rkv)
    xw, xa, xg = xs[3], xs[4], xs[5]
    w_log = -jax.nn.softplus(-(w0 + jnp.tanh(xw @ w1) @ w2).astype(jnp.float32)) - 0.5
    decay = jnp.exp(-jnp.exp(w_log))
    a = jax.nn.sigmoid(a0 + (xa @ a1) @ a2)
    g = jax.nn.sigmoid(xg @ g1) @ g2
    kk = (k * k_k).reshape(B, S, H, N).astype(jnp.float32)
    kk = kk / jnp.maximum(jnp.linalg.norm(kk, axis=-1, keepdims=True), 1e-12)
    k = k * (1 + (a - 1) * k_a)
    if v_res is None:
        v_first = v
    else:
        v0, v1, v2 = v_res
        v = v + (v_first - v) * jax.nn.sigmoid(v0 + (xs[2] @ v1) @ v2)
    rh, kh, vh = (t.reshape(B, S, H, N) for t in (r, k, v))
    ah = a.reshape(B, S, H, N).astype(jnp.float32)
    y = wkv7_scan(rh, decay.reshape(B, S, H, N), kh, vh, -kk, kk * ah)
    mean = jnp.mean(y, axis=-1, keepdims=True)
    var = jnp.mean(jnp.square(y - mean), axis=-1, keepdims=True)
    y = ((y - mean) * lax.rsqrt(var + GN_EPS)).reshape(B, S, D)
    y = y * ln_w.astype(jnp.float32) + ln_b.astype(jnp.float32)
    bonus = jnp.sum(rh.astype(jnp.float32) * kh.astype(jnp.float32) * r_k.astype(jnp.float32),
                    axis=-1, keepdims=True) * vh.astype(jnp.float32)
    y = (y + bonus.reshape(B, S, D)).astype(x.dtype)
    return (y * g) @ w_o, v_first


def setup_inputs(seed: int = 0) -> dict:
    key = jax.random.key(seed)
    ks = iter(jax.random.split(key, 40))
    nrm = lambda shape, scale: jax.random.normal(next(ks), shape, jnp.float32) * scale
    uni = lambda shape, lo, hi: jax.random.uniform(next(ks), shape, jnp.float32, lo, hi)
    D, F, H, N = D_MODEL, D_FF, N_HEADS, HEAD_SIZE
    R = N_RWKV
    return {
        "x": nrm((BATCH, SEQ, D), 1.0),
        "norm_g": 1.0 + nrm((DEPTH, 3, D), 0.02),
        "final_g": 1.0 + nrm((D,), 0.02),
        "ffn_w_gu": nrm((DEPTH, 2, D, 2 * F), D ** -0.5),
        "ffn_w_d": nrm((DEPTH, 2, F, D), F ** -0.5),
        "conv_w_in": nrm((N_CONV, D, 3 * D), D ** -0.5),
        "conv_w": nrm((N_CONV, CONV_W, D), CONV_W ** -0.5),
        "conv_w_out": nrm((N_CONV, D, D), D ** -0.5),
        "rwkv_mu": uni((R, N_MIX, D), 0.0, 1.0),
        "rwkv_w_rkv": nrm((R, 3, D, D), D ** -0.5),
        "rwkv_w0": uni((R, D), -5.0, 1.0),
        "rwkv_w1": nrm((R, D, D_DECAY_LORA), D ** -0.5),
        "rwkv_w2": nrm((R, D_DECAY_LORA, D), 0.1 * D_DECAY_LORA ** -0.5),
        "rwkv_a0": nrm((R, D), 0.1),
        "rwkv_a1": nrm((R, D, D_AAA_LORA), D ** -0.5),
        "rwkv_a2": nrm((R, D_AAA_LORA, D), 0.1 * D_AAA_LORA ** -0.5),
        "rwkv_g1": nrm((R, D, D_GATE_LORA), D ** -0.5),
        "rwkv_g2": nrm((R, D_GATE_LORA, D), D_GATE_LORA ** -0.5),
        "rwkv_k_k": 0.85 + nrm((R, D), 0.02),
        "rwkv_k_a": 1.0 + nrm((R, D), 0.02),
        "rwkv_r_k": nrm((R, H, N), 0.1),
        "rwkv_ln_w": 1.0 + nrm((R, D), 0.02),
        "rwkv_ln_b": nrm((R, D), 0.02),
        "rwkv_w_o": nrm((R, D, D), D ** -0.5),
        "rwkv_v0": 1.0 + nrm((N_VRES, D), 0.1),
        "rwkv_v1": nrm((N_VRES, D, D_MV_LORA), D ** -0.5),
        "rwkv_v2": nrm((N_VRES, D_MV_LORA, D), 0.1 * D_MV_LORA ** -0.5),
    }


def reference(x, norm_g, final_g, ffn_w_gu, ffn_w_d, conv_w_in, conv_w, conv_w_out,
              rwkv_mu, rwkv_w_rkv, rwkv_w0, rwkv_w1, rwkv_w2, rwkv_a0, rwkv_a1, rwkv_a2,
              rwkv_g1, rwkv_g2, rwkv_k_k, rwkv_k_a, rwkv_r_k, rwkv_ln_w, rwkv_ln_b, rwkv_w_o,
              rwkv_v0, rwkv_v1, rwkv_v2):
    v_first = None
    for i in range(DEPTH):
        x = x + 0.5 * swiglu(rms_norm(x, norm_g[i, 0]), ffn_w_gu[i, 0], ffn_w_d[i, 0])
        h = rms_norm(x, norm_g[i, 1])
        if i % 2 == 0:
            c = i // 2
            x = x + short_conv_mixer(h, conv_w_in[c], conv_w[c], conv_w_out[c])
        else:
            j = i // 2
            v_res = None if j == 0 else (rwkv_v0[j - 1], rwkv_v1[j - 1], rwkv_v2[j - 1])
            out, v_first = rwkv7_mixer(
                h, v_first, rwkv_mu[j], rwkv_w_rkv[j], rwkv_w0[j], rwkv_w1[j], rwkv_w2[j],
                rwkv_a0[j], rwkv_a1[j], rwkv_a2[j], rwkv_g1[j], rwkv_g2[j],
                rwkv_k_k[j], rwkv_k_a[j], rwkv_r_k[j], rwkv_ln_w[j], rwkv_ln_b[j], rwkv_w_o[j], v_res)
            x = x + out
        x = x + 0.5 * swiglu(rms_norm(x, norm_g[i, 2]), ffn_w_gu[i, 1], ffn_w_d[i, 1])
    return rms_norm(x, final_g)
```

```python
import numpy as np
import concourse.bass as bass
import concourse.mybir as mybir

F32 = mybir.dt.float32
BF16 = mybir.dt.bfloat16
AF = mybir.ActivationFunctionType
ALU = mybir.AluOpType

SAME_ENGINE_SYNC = True


class Buf:
    __slots__ = ("t", "name", "last_write", "readers", "excl")

    def __init__(self, t, name, excl=False):
        self.t = t
        self.name = name
        self.excl = excl
        self.last_write = None
        self.readers = {}

    def __getitem__(self, idx):
        return self.t[idx]


class Prog:
    ENG = ("pe", "act", "dve", "pool", "sp")
    MAXV = 3900

    def __init__(self, nc):
        self.nc = nc
        self.q = {e: [] for e in self.ENG}
        self.cnt = {}
        self.waited = {e: {} for e in self.ENG}
        self.sems = {}
        self._ctx = []
        self.cur = {}
        self.alias = {}
        self.epoch = {}
        self.refs = {}
        for e in self.ENG:
            self.new_epoch(e)
        self.n_wait = 0

    def new_sem(self, key):
        cm = self.nc.semaphore(str(key).replace("@", "_"))
        s = cm.__enter__()
        self._ctx.append(cm)
        self.sems[key] = s
        self.cnt[key] = 0
        self.refs[key] = set()
        return key

    def new_epoch(self, name):
        e = self.epoch.get(name, -1) + 1
        self.epoch[name] = e
        key = "%s@%d" % (name, e)
        self.new_sem(key)
        self.cur[name] = key
        return key

    def sbuf(self, name, shape, dt):
        cm = self.nc.sbuf_tensor(name, list(shape), dt)
        t = cm.__enter__()
        self._ctx.append(cm)
        return Buf(t, name)

    def psum(self, name, shape, dt):
        cm = self.nc.psum_tensor(name, list(shape), dt)
        t = cm.__enter__()
        self._ctx.append(cm)
        return Buf(t, name, excl=True)

    def close(self):
        for cm in reversed(self._ctx):
            cm.__exit__(None, None, None)
        self._ctx = []

    def _deps(self, eng, reads, writes):
        need = {}

        def add(tk):
            if tk is None:
                return
            k, v = tk
            if need.get(k, 0) < v:
                need[k] = v

        for r in reads:
            add(r.last_write)
            if r.excl:
                for k, v in r.readers.items():
                    if k.split("@")[0] != eng:
                        add((k, v))
        for w in writes:
            add(w.last_write)
            for k, v in w.readers.items():
                add((k, v))
        waits = []
        for k, v in need.items():
            if k.split("@")[0] == eng and (eng == "pe" or not SAME_ENGINE_SYNC):
                continue
            if self.waited[eng].get(k, 0) >= v:
                continue
            self.waited[eng][k] = v
            self.refs[k].add(v)
            waits.append((k, v))
        return waits

    def _mark(self, ticket, reads, writes):
        k, v = ticket
        for r in reads:
            if r.readers.get(k, 0) < v:
                r.readers[k] = v
        for w in writes:
            w.last_write = ticket
            w.readers = {}

    def op(self, eng, meth, *args, R=(), W=(), **kw):
        fn = (meth, args, kw)
        waits = self._deps(eng, R, W)
        key = self.cur[eng]
        self.cnt[key] += 1
        ticket = (key, self.cnt[key])
        self.q[eng].append((waits, fn, ticket))
        self._mark(ticket, R, W)
        self.n_wait += len(waits)
        return ticket

    def dma(self, queue, semname, R=(), W=(), **kw):
        fn = ("dma_start", (), kw)
        waits = self._deps(queue, R, W)
        key = self.cur[semname]
        self.cnt[key] += 16
        assert self.cnt[key] <= self.MAXV, ("dma semaphore too large", key, self.cnt[key])
        ticket = (key, self.cnt[key])
        self.q[queue].append((waits, fn, ticket))
        self._mark(ticket, R, W)
        return ticket

    def tick(self, semname):
        key = self.cur[semname]
        return (key, self.cnt[key])

    def wait(self, eng, ticket):
        k, v = ticket
        if v <= 0 or self.waited[eng].get(k, 0) >= v:
            return
        self.waited[eng][k] = v
        self.refs[k].add(v)
        self.q[eng].append(([(k, v)], None, None))

    def emit(self):
        nc = self.nc
        sems = self.sems
        q = self.q
        rank = {}
        for k, rs in self.refs.items():
            if k.split("@")[0] in self.ENG:
                rank[k] = {v: i + 1 for i, v in enumerate(sorted(rs))}
                assert len(rs) <= self.MAXV, ("engine semaphore too large", k, len(rs))

        def val(k, v):
            return rank[k][v] if k in rank else v

        with nc.Block() as block:
            def run(engobj, lst):
                for waits, fn, tk in lst:
                    for k, v in waits:
                        engobj.wait_ge(sems[k], val(k, v))
                    if fn is not None:
                        ins = getattr(engobj, fn[0])(*fn[1], **fn[2])
                        k, v = tk
                        if k in rank:
                            if v in rank[k]:
                                ins.then_inc(sems[k], 1)
                        else:
                            ins.then_inc(sems[k], 16)

            names = {"pe": "tensor", "act": "scalar", "dve": "vector", "pool": "gpsimd", "sp": "sync"}
            for k, bn in names.items():
                if not q[k]:
                    continue
                getattr(block, bn)(lambda e, k=k: run(e, q[k]))

from concourse.bass_utils import run_bass_kernel_spmd

D = 1024; NCH = 8; FF = 2816; NFC = 22; T = 512; SEQ = 8192; DEPTH = 4
EPS = 1e-5
PIECE = 4096

VEC_LAYOUT = [("norm_g", 12), ("final_g", 1), ("conv_w", 6), ("mu", 12), ("w0", 2), ("a0", 2), ("k_k", 2),
              ("k_a", 2), ("r_k", 2), ("ln_w", 2), ("ln_b", 2), ("v0", 1)]
VOFF = {}
_o = 0
for _n, _c in VEC_LAYOUT:
    VOFF[_n] = _o
    _o += _c * NCH
NVEC = _o


def gu_pieces():
    out = []
    for j in range(6):
        w = 512 if j < 5 else 256
        out.append((j * 512, w)); out.append((FF + j * 512, w))
    return out


class Ctx:
    pass


def build(nt=16, nlayers=4, do_rwkv=True, ring_n=4):
    nc = bass.Bass("TRN2", target_bir_lowering=False)
    P = Prog(nc)
    C = Ctx(); C.nc = nc; C.P = P
    dt_in = lambda name, shape: nc.dram_tensor(name, list(shape), F32, kind="ExternalInput").ap()
    x_d = dt_in("x", [SEQ, D])
    vec_d = dt_in("vecs_in", [128, NVEC])
    wgu_d = dt_in("ffn_w_gu", [8, D, 2 * FF])
    wd_d = dt_in("ffn_w_d", [8, FF, D])
    cwin_d = dt_in("conv_w_in", [2, D, 3 * D])
    cwout_d = dt_in("conv_w_out", [2, D, D])
    wrkv_d = dt_in("rwkv_w_rkv", [6, D, D])
    wo_d = dt_in("rwkv_w_o", [2, D, D])
    w1_d = dt_in("rwkv_w1", [2, D, 64]); w2_d = dt_in("rwkv_w2", [2, 64, D])
    a1_d = dt_in("rwkv_a1", [2, D, 64]); a2_d = dt_in("rwkv_a2", [2, 64, D])
    g1_d = dt_in("rwkv_g1", [2, D, 160]); g2_d = dt_in("rwkv_g2", [2, 160, D])
    v1_d = dt_in("rwkv_v1", [1, D, 32]); v2_d = dt_in("rwkv_v2", [1, 32, D])
    out_d = nc.dram_tensor("out", [SEQ, D], F32, kind="ExternalOutput").ap()

    def scratch(name, npieces):
        t = nc.dram_tensor(name, [npieces, 128, PIECE], BF16).ap()
        return Buf(t, name)
    S_gu = scratch("s_gu", 8 * 12)
    S_d = scratch("s_d", 8 * 8)
    S_cin = scratch("s_cin", 2 * 6)
    S_cout = scratch("s_cout", 2 * 2)
    S_rkv = scratch("s_rkv", 6 * 2)
    S_o = scratch("s_o", 2 * 2)
    S_lora = scratch("s_lora", 2 * 8)
    scr = [S_gu, S_d, S_cin, S_cout, S_rkv, S_o, S_lora]

    for s in ["cast", "xld", "xst", "cst"] + ["ring%d" % i for i in range(ring_n)]:
        P.new_epoch(s)

    sb = P.sbuf
    vecs = sb("vecs", [128, NVEC], F32)
    ident_f = sb("ident_f", [128, 128], F32)
    ident_b = sb("ident_b", [128, 128], BF16)
    ones_b = sb("ones_b", [128, 128], BF16)
    xT = [sb("xT%d" % c, [128, T], F32) for c in range(NCH)]
    hT = sb("hT", [128, NCH, T + 1], BF16)
    hid = sb("hid", [128, NFC, T], BF16)
    xio = sb("xio", [128, 4, D], F32)
    ring = [sb("rbuf%d" % i, [128, PIECE], BF16) for i in range(ring_n)]
    rstd = sb("rstd", [128, T], F32)
    sg = [sb("sg%d" % i, [128, T], F32) for i in range(2)]
    ubuf = sb("ubuf", [128, NCH, T], BF16)
    zbuf = [sb("zbuf%d" % i, [128, T + 2], F32) for i in range(2)]
    tbuf = [sb("tbuf%d" % i, [128, T], F32) for i in range(2)]
    zst = [sb("zst%d" % i, [128, NCH, 2], F32) for i in range(2)]
    ps = [P.psum("ps%d" % i, [128, T], F32) for i in range(8)]
    C.ps = ps; C.ps_i = 0

    def next_ps():
        b = ps[C.ps_i % 8]; C.ps_i += 1
        return b

    def vcol(name, idx, c):
        o = VOFF[name] + idx * NCH + c
        return vecs[:, o:o + 1]

    op = P.op
    P.dma("sp", "cst", out=vecs[:], in_=vec_d[:, :], W=[vecs])
    op("pool", "memset", ident_f[:], 0.0, W=[ident_f])
    op("pool", "affine_select", out=ident_f[:], in_=ident_f[:], pattern=[[-1, 128]], compare_op=ALU.not_equal,
       fill=1.0, base=0, channel_multiplier=1, R=[ident_f], W=[ident_f])
    op("dve", "tensor_copy", out=ident_b[:], in_=ident_f[:], R=[ident_f], W=[ident_b])
    op("dve", "memset", ones_b[:], 1.0, W=[ones_b])
    for z in zst:
        op("dve", "memset", z[:], 0.0, W=[z])
    op("dve", "memset", hT[:], 0.0, W=[hT])

    C.cast_done = []

    def cast_roll():
        k_, v_ = P.tick("cast")
        if v_ >= 2048:
            C.cast_done.append((k_, v_))
            P.new_epoch("cast")
    C.cast_roll = cast_roll

    def cast_piece(dst, pi, src2d, col0, width, nk=8):
        o = dst.t[pi, :, 0:nk * width].rearrange("p (k n) -> p k n", k=nk)
        i = src2d[:, col0:col0 + width].rearrange("(k p) n -> p k n", p=128)
        n_out = 3 if nk > 8 else 6
        cast_roll()
        k_, v_ = P.tick("cast")
        if v_ - 16 * n_out > 0:
            P.wait("pool", (k_, v_ - 16 * n_out))
        P.dma("pool", "cast", out=o, in_=i, W=[dst])

    for ln in range(nlayers * 2):
        for pi, (c0, w) in enumerate(gu_pieces()):
            cast_piece(S_gu, ln * 12 + pi, wgu_d[ln], c0, w)
        for oc in range(8):
            cast_piece(S_d, ln * 8 + oc, wd_d[ln], oc * 128, 128, nk=NFC)
    for cl in range((nlayers + 1) // 2):
        for pi in range(6):
            cast_piece(S_cin, cl * 6 + pi, cwin_d[cl], pi * 512, 512)
        for pi in range(2):
            cast_piece(S_cout, cl * 2 + pi, cwout_d[cl], pi * 512, 512)
    if do_rwkv:
        for rl in range(nlayers // 2):
            for m in range(3):
                for pi in range(2):
                    cast_piece(S_rkv, (rl * 3 + m) * 2 + pi, wrkv_d[rl * 3 + m], pi * 512, 512)
            for pi in range(2):
                cast_piece(S_o, rl * 2 + pi, wo_d[rl], pi * 512, 512)
    C.cast_piece = cast_piece

    C.ring_i = 0

    def load_piece(src, pi, nelem, rows=128, extra=None):
        slot = ring[C.ring_i % ring_n]; sem = "ring%d" % (C.ring_i % ring_n); C.ring_i += 1
        P.dma("sp", sem, out=slot[0:rows, 0:nelem], in_=src.t[pi, 0:rows, 0:nelem], R=[src], W=[slot])
        if extra is not None:
            r2, c0, c1 = extra
            P.dma("sp", sem, out=slot[0:r2, c0:c1], in_=src.t[pi, 0:r2, c0:c1], R=[src], W=[slot])
        return slot

    def rmsnorm(gname, gidx, out_kind="hT"):
        for c in range(NCH):
            op("act", "activation", out=hT[:, c, 1:T + 1], in_=xT[c][:], func=AF.Square, R=[xT[c]], W=[hT])
        b = next_ps()
        for c in range(NCH):
            op("pe", "matmul", b[:], lhsT=ones_b[:], rhs=hT[:, c, 1:T + 1], start=(c == 0), stop=(c == NCH - 1), R=[ones_b, hT], W=[b])
        op("act", "activation", out=rstd[:], in_=b[:], func=AF.Sqrt, bias=EPS, scale=1.0 / D, R=[b], W=[rstd])
        op("dve", "reciprocal", out=rstd[:], in_=rstd[:], R=[rstd], W=[rstd])
        for c in range(NCH):
            if out_kind == "hT":
                o = hT[:, c, 1:T + 1]; wr = [hT]
            else:
                o = xT[c][:]; wr = [xT[c]]
            op("dve", "scalar_tensor_tensor", out=o, in0=xT[c][:], scalar=vcol(gname, gidx, c), in1=rstd[:],
               op0=ALU.mult, op1=ALU.mult, R=[xT[c], vecs, rstd], W=wr)

    def mm_acc(b, sl, w, q, rhs_buf, rhs_of, n):
        for kc in range(n):
            op("pe", "matmul", b[:], lhsT=sl[:, kc * w + q * 128: kc * w + (q + 1) * 128], rhs=rhs_of(kc),
               start=(kc == 0), stop=(kc == n - 1), R=[sl, rhs_buf], W=[b])

    def ffn(ln):
        rmsnorm("norm_g", (ln // 2) * 3 + (0 if ln % 2 == 0 else 2))
        pieces = gu_pieces()
        for j in range(6):
            w = pieces[2 * j][1]
            sl_g = load_piece(S_gu, ln * 12 + 2 * j, 8 * w)
            sl_u = load_piece(S_gu, ln * 12 + 2 * j + 1, 8 * w)
            for q in range(w // 128):
                f = j * 4 + q
                bg = next_ps(); bu = next_ps()
                mm_acc(bg, sl_g, w, q, hT, lambda kc: hT[:, kc, 1:T + 1], NCH)
                mm_acc(bu, sl_u, w, q, hT, lambda kc: hT[:, kc, 1:T + 1], NCH)
                s_ = sg[f % 2]
                op("act", "activation", out=s_[:], in_=bg[:], func=AF.Sigmoid, R=[bg], W=[s_])
                op("dve", "tensor_tensor", out=s_[:], in0=s_[:], in1=bg[:], op=ALU.mult, R=[s_, bg], W=[s_])
                op("dve", "tensor_tensor", out=hid[:, f, :], in0=s_[:], in1=bu[:], op=ALU.mult, R=[s_, bu], W=[hid])
        for oc in range(NCH):
            sl = load_piece(S_d, ln * 8 + oc, NFC * 128)
            b = next_ps()
            mm_acc(b, sl, 128, 0, hid, lambda f: hid[:, f, :], NFC)
            op("dve", "scalar_tensor_tensor", out=xT[oc][:], in0=b[:], scalar=0.5, in1=xT[oc][:],
               op0=ALU.mult, op1=ALU.add, R=[b, xT[oc]], W=[xT[oc]])

    def conv_mixer(li):
        cl = li // 2
        rmsnorm("norm_g", li * 3 + 1)
        zs = zst[cl]
        for half in range(2):
            sl = [load_piece(S_cin, cl * 6 + g * 2 + half, 8 * 512) for g in range(3)]
            for q in range(4):
                fc = half * 4 + q
                bk = [next_ps() for _ in range(3)]
                for g in range(3):
                    mm_acc(bk[g], sl[g], 512, q, hT, lambda kc: hT[:, kc, 1:T + 1], NCH)
                zb = zbuf[fc % 2]; tb = tbuf[fc % 2]; s_ = sg[fc % 2]
                op("act", "activation", out=s_[:], in_=bk[1][:], func=AF.Copy, R=[bk[1]], W=[s_])
                op("dve", "tensor_copy", out=zb[:, 0:2], in_=zs[:, fc, :], R=[zs], W=[zb])
                op("dve", "tensor_tensor", out=zb[:, 2:T + 2], in0=s_[:], in1=bk[2][:], op=ALU.mult, R=[s_, bk[2]], W=[zb])
                op("dve", "tensor_copy", out=zs[:, fc, :], in_=zb[:, T:T + 2], R=[zb], W=[zs])
                op("dve", "tensor_scalar", out=tb[:], in0=zb[:, 2:T + 2], scalar1=vcol("conv_w", cl * 3 + 2, fc), scalar2=None,
                   op0=ALU.mult, R=[zb, vecs], W=[tb])
                op("dve", "scalar_tensor_tensor", out=tb[:], in0=zb[:, 1:T + 1], scalar=vcol("conv_w", cl * 3 + 1, fc), in1=tb[:],
                   op0=ALU.mult, op1=ALU.add, R=[zb, vecs, tb], W=[tb])
                op("dve", "scalar_tensor_tensor", out=tb[:], in0=zb[:, 0:T], scalar=vcol("conv_w", cl * 3 + 0, fc), in1=tb[:],
                   op0=ALU.mult, op1=ALU.add, R=[zb, vecs, tb], W=[tb])
                op("dve", "tensor_tensor", out=ubuf[:, fc, :], in0=tb[:], in1=bk[0][:], op=ALU.mult, R=[tb, bk[0]], W=[ubuf])
        out_proj(S_cout, cl * 2)

    def out_proj(S, pbase):
        for half in range(2):
            sl = load_piece(S, pbase + half, 8 * 512)
            for q in range(4):
                oc = half * 4 + q
                b = next_ps()
                mm_acc(b, sl, 512, q, ubuf, lambda kc: ubuf[:, kc, :], NCH)
                op("dve", "tensor_tensor", out=xT[oc][:], in0=xT[oc][:], in1=b[:], op=ALU.add, R=[xT[oc], b], W=[xT[oc]])

    def load_x(ti):
        src = x_d[ti * T:(ti + 1) * T, :].rearrange("(s p) d -> p s d", p=128)
        P.dma("pool", "xld", out=xio[:], in_=src, W=[xio])
        for c in range(NCH):
            b = next_ps()
            for s in range(4):
                op("pe", "transpose", b[:, s * 128:(s + 1) * 128], xio[:, s, c * 128:(c + 1) * 128], ident_f[:], R=[xio, ident_f], W=[b])
            if c % 2 == 0:
                op("act", "activation", out=xT[c][:], in_=b[:], func=AF.Copy, R=[b], W=[xT[c]])
            else:
                op("dve", "tensor_copy", out=xT[c][:], in_=b[:], R=[b], W=[xT[c]])

    def store_x(ti):
        rmsnorm("final_g", 0, out_kind="xT")
        for s in range(4):
            for half in range(2):
                b = next_ps()
                for q in range(4):
                    c = half * 4 + q
                    op("pe", "transpose", b[:, q * 128:(q + 1) * 128], xT[c][:, s * 128:(s + 1) * 128], ident_f[:], R=[xT[c], ident_f], W=[b])
                if half == 0:
                    op("act", "activation", out=xio[:, s, half * 512:(half + 1) * 512], in_=b[:], func=AF.Copy, R=[b], W=[xio])
                else:
                    op("dve", "tensor_copy", out=xio[:, s, half * 512:(half + 1) * 512], in_=b[:], R=[b], W=[xio])
        dst = out_d[ti * T:(ti + 1) * T, :].rearrange("(s p) d -> p s d", p=128)
        P.dma("pool", "xst", out=dst, in_=xio[:], R=[xio])

    import os as _os
    C.dbg = _os.environ.get("MK_DBG", "ffn,conv,rwkv").split(",")
    C.rwkv = None
    if do_rwkv:
        C.rwkv = make_rwkv(C, locals())
    C.cast_done.append(P.tick("cast"))
    for s_ in scr:
        s_.last_write = None
    for tk_ in C.cast_done:
        P.wait("sp", tk_)

    import os as _os
    dbg = _os.environ.get("MK_DBG", "ffn,conv,rwkv").split(",")
    for ti in range(nt):
        if ti > 0:
            for e_ in ("pe", "act", "dve"):
                P.new_epoch(e_)
            if ti % 4 == 0:
                for i_ in range(ring_n):
                    P.new_epoch("ring%d" % i_)
        load_x(ti)
        for li in range(nlayers):
            if "ffn" in dbg:
                ffn(li * 2)
            if li % 2 == 0:
                if "conv" in dbg:
                    conv_mixer(li)
            elif do_rwkv and "rwkv" in dbg:
                C.rwkv(li, ti)
            if "ffn" in dbg:
                ffn(li * 2 + 1)
        store_x(ti)
    P.wait("pool", P.tick("xst"))
    P.emit()
    P.close()
    return nc


def make_rwkv(C, L):
    P = C.P; op = P.op; nc = C.nc
    g = lambda n: L[n]
    hT, xT, vecs, vcol, ubuf, hid = g("hT"), g("xT"), g("vecs"), g("vcol"), g("ubuf"), g("hid")
    ident_b, next_ps, load_piece, mm_acc = g("ident_b"), g("next_ps"), g("load_piece"), g("mm_acc")
    S_rkv, S_o, S_lora, rmsnorm, out_proj = g("S_rkv"), g("S_o"), g("S_lora"), g("rmsnorm"), g("out_proj")
    sgt, zbuf, tbuf, rstd = g("sg"), g("zbuf"), g("tbuf"), g("rstd")
    nlayers = g("nlayers")
    sb = P.sbuf
    def cst(o, i):
        C.cast_roll()
        P.dma("pool", "cast", out=o, in_=i, W=[S_lora])
    v3 = lambda d: d.rearrange("(k p) n -> p k n", p=128)
    blk = lambda pi, off, n, k=8: S_lora.t[pi, :, off:off + n].rearrange("p (k n) -> p k n", k=k)
    for rl in range(nlayers // 2):
        pA1, pA2, pB, pC = rl * 4, rl * 4 + 1, rl * 4 + 2, rl * 4 + 3
        for j in range(2):
            cst(blk(pA1, 0, 1024)[:, :, j * 64:(j + 1) * 64], v3(g("w1_d")[rl]))
            cst(blk(pA1, 1024, 1024)[:, :, j * 64:(j + 1) * 64], v3(g("a1_d")[rl]))
            cst(S_lora.t[pB, j * 64:(j + 1) * 64, 0:1024], g("w2_d")[rl])
            cst(S_lora.t[pB, j * 64:(j + 1) * 64, 1024:2048], g("a2_d")[rl])
        cst(blk(pA2, 0, 2048)[:, :, 0:128], v3(g("g1_d")[rl])[:, :, 0:128])
        cst(S_lora.t[pC, :, 0:1024], g("g2_d")[rl][0:128, :])
        for j in range(4):
            cst(blk(pA2, 0, 2048)[:, :, 128 + j * 32:128 + (j + 1) * 32], v3(g("g1_d")[rl])[:, :, 128:160])
            cst(S_lora.t[pC, j * 32:(j + 1) * 32, 1024:2048], g("g2_d")[rl][128:160, :])
            if rl == 1:
                cst(blk(pA1, 2048, 1024)[:, :, j * 32:(j + 1) * 32], v3(g("v1_d")[0]))
                cst(S_lora.t[pB, j * 32:(j + 1) * 32, 2048:3072], g("v2_d")[0])
    onesBD = sb("onesBD", [128, 128], BF16)
    mSU = sb("mSU", [128, 128], F32); m3 = sb("m3", [128, 384], F32); mSL = sb("mSL", [128, 128], F32)
    SM = sb("SM", [128, T], F32)
    kam1 = sb("kam1", [128, 16], F32)
    op("dve", "memset", onesBD[:], 0.0, W=[onesBD])
    op("dve", "memset", onesBD[0:64, 0:64], 1.0, W=[onesBD])
    op("dve", "memset", onesBD[64:128, 64:128], 1.0, W=[onesBD])

    def tri(dst, col0, cmp_op, pat, cm, zero_rows, zero_cols):
        o = dst[:, col0:col0 + 128]
        op("dve", "memset", o, 1.0, W=[dst])
        op("pool", "affine_select", out=o, in_=o, pattern=[[pat, 128]], compare_op=cmp_op, fill=0.0, base=0, channel_multiplier=cm, R=[dst], W=[dst])
        op("dve", "memset", dst[zero_rows[0]:zero_rows[1], col0 + zero_cols[0]:col0 + zero_cols[1]], 0.0, W=[dst])
    tri(mSU, 0, ALU.is_gt, 1, -1, (0, 64), (64, 128))
    tri(m3, 0, ALU.is_ge, 1, -1, (0, 64), (64, 128))
    tri(m3, 128, ALU.is_gt, 1, -1, (0, 64), (64, 128))
    tri(m3, 256, ALU.is_ge, 1, -1, (0, 64), (64, 128))
    tri(mSL, 0, ALU.is_gt, -1, 1, (64, 128), (0, 64))
    op("dve", "memset", SM[:], 1.0, W=[SM])
    op("dve", "memset", SM[:].rearrange("p (c t) -> p c t", t=64)[:, :, 0:1], 0.0, W=[SM])
    mk = sb("hmask", [128, 4], F32)
    op("dve", "memset", mk[:], 0.0, W=[mk])
    op("dve", "memset", mk[0:64, 0:1], 1.0, W=[mk])
    op("dve", "memset", mk[64:128, 1:2], 1.0, W=[mk])
    op("dve", "tensor_scalar", out=mk[:, 2:4], in0=mk[:, 0:2], scalar1=-1.0, scalar2=None, op0=ALU.mult, R=[mk], W=[mk])
    ko = VOFF["k_a"]
    op("dve", "tensor_scalar", out=kam1[:], in0=vecs[:, ko:ko + 16], scalar1=-1.0, scalar2=1.0, op0=ALU.mult, op1=ALU.add, R=[vecs], W=[kam1])
    nr = max(nlayers // 2, 1)
    S_f = [[sb("Sf%d_%d" % (r, c), [128, 128], F32) for c in range(8)] for r in range(nr)]
    S_b = [[sb("Sb%d_%d" % (r, c), [128, 128], BF16) for c in range(8)] for r in range(nr)]
    hst = [sb("hst%d" % r, [128, NCH, 1], BF16) for r in range(nr)]
    for r in range(nr):
        for c in range(8):
            op("dve", "memset", S_f[r][c][:], 0.0, W=[S_f[r][c]])
            op("dve", "memset", S_b[r][c][:], 0.0, W=[S_b[r][c]])
        op("dve", "memset", hst[r][:], 0.0, W=[hst[r]])
    vfirst = sb("vfirst", [128, NCH, T], BF16)
    xs = sb("xs", [128, NCH, T], BF16)
    R_ = sb("R_", [128, 4, T], BF16); K_ = sb("K_", [128, 4, T], BF16); V_ = sb("V_", [128, 4, T], BF16)
    A_ = sb("A_", [128, 4, T], BF16); G_ = sb("G_", [128, 4, T], BF16); LW_ = sb("LW_", [128, 4, T], F32)
    lo128 = sb("lo128", [128, T], BF16); lo64 = sb("lo64", [128, T], BF16); lo32 = sb("lo32", [128, T], BF16)
    for t_ in (lo64, lo32):
        op("dve", "memset", t_[:], 0.0, W=[t_])
    BD_ar = sb("BD_ar", [128, 8, 2, 128], BF16)
    BD_b = sb("BD_b", [128, 8, 128], BF16); BD_k = sb("BD_k", [128, 8, 128], BF16); BD_v = sb("BD_v", [128, 8, 128], BF16)
    for t_ in (BD_ar, BD_b, BD_k, BD_v):
        op("dve", "memset", t_[:], 0.0, W=[t_])
    rhsb = [sb("rhsb%d" % i, [128, 128], BF16) for i in range(2)]
    sab = [sb("sab%d" % i, [128, 128], BF16) for i in range(2)]
    hv = hid.t
    AM = [Buf(hv[:, c8, 0:384], "AM%d" % c8) for c8 in range(8)]
    PQ = [Buf(hv[:, 8 + c8 // 2, (c8 % 2) * 256:(c8 % 2) * 256 + 256], "PQ%d" % c8) for c8 in range(8)]
    XB = [Buf(hv[:, 12 + c8 // 4, (c8 % 4) * 128:(c8 % 4) * 128 + 128], "XB%d" % c8) for c8 in range(8)]
    TM = [Buf(hv[:, 14 + c8, 0:384], "TM%d" % c8) for c8 in range(8)]
    F = [sgt[0], sgt[1], tbuf[0], tbuf[1], rstd]
    Z0, Z1 = zbuf
    NEG_E = -0.6065306597126334

    def half_view(buf_ap):
        return buf_ap.rearrange("p (c t) -> p c t", t=64)

    def rwkv(li, ti):
        rl = li // 2
        pA1, pA2, pB, pC = rl * 4, rl * 4 + 1, rl * 4 + 2, rl * 4 + 3
        op("dve", "tensor_copy", out=hT[:, :, 0:1], in_=hst[rl][:], R=[hst[rl]], W=[hT])
        rmsnorm("norm_g", li * 3 + 1)
        op("dve", "tensor_copy", out=hst[rl][:], in_=hT[:, :, T:T + 1], R=[hT], W=[hst[rl]])

        def mix(m):
            for c in range(NCH):
                tmp = F[c % 2]
                op("dve", "tensor_tensor", out=tmp[:], in0=hT[:, c, 0:T], in1=hT[:, c, 1:T + 1], op=ALU.subtract, R=[hT], W=[tmp])
                op("dve", "scalar_tensor_tensor", out=xs[:, c, :], in0=tmp[:], scalar=vcol("mu", rl * 6 + m, c), in1=hT[:, c, 1:T + 1],
                   op0=ALU.mult, op1=ALU.add, R=[tmp, vecs, hT], W=[xs])
        xs_of = lambda kc: xs[:, kc, :]

        def stage1(sl, off, blkw, col0, M, dst, func):
            b = next_ps()
            for kc in range(NCH):
                o_ = off + kc * blkw + col0
                op("pe", "matmul", b[:], lhsT=sl[:, o_:o_ + 128], rhs=xs[:, kc, :],
                   start=(kc == 0), stop=(kc == NCH - 1), R=[sl, xs], W=[b])
            op("act", "activation", out=dst[0:M, :], in_=b[0:M, :], func=func, R=[b], W=[dst])

        for half in range(2):
            for m, dst in ((0, R_), (1, K_)):
                mix(m)
                W = load_piece(S_rkv, (rl * 3 + m) * 2 + half, 8 * 512)
                for q in range(4):
                    b = next_ps()
                    mm_acc(b, W, 512, q, xs, xs_of, NCH)
                    op("act", "activation", out=dst[:, q, :], in_=b[:], func=AF.Copy, R=[b], W=[dst])
            slA = load_piece(S_lora, pA1, 3072 if rl == 1 else 2048)
            slB = load_piece(S_lora, pB, 3072 if rl == 1 else 2048)
            mix(2)
            W = load_piece(S_rkv, (rl * 3 + 2) * 2 + half, 8 * 512)
            if rl == 1:
                stage1(slA, 2048, 128, 0, 32, lo32, AF.Copy)
            for q in range(4):
                c = half * 4 + q
                b = next_ps()
                mm_acc(b, W, 512, q, xs, xs_of, NCH)
                if rl == 0:
                    op("act", "activation", out=V_[:, q, :], in_=b[:], func=AF.Copy, R=[b], W=[V_])
                    op("dve", "tensor_copy", out=vfirst[:, c, :], in_=V_[:, q, :], R=[V_], W=[vfirst])
                else:
                    b2 = next_ps()
                    op("pe", "matmul", b2[:], lhsT=slB[:, 2048 + c * 128:2048 + (c + 1) * 128], rhs=lo32[:], start=True, stop=True, R=[slB, lo32], W=[b2])
                    gt = F[2]; dd = F[3]
                    op("act", "activation", out=gt[:], in_=b2[:], func=AF.Sigmoid, bias=vcol("v0", 0, c), R=[b2, vecs], W=[gt])
                    op("dve", "tensor_tensor", out=dd[:], in0=vfirst[:, c, :], in1=b[:], op=ALU.subtract, R=[vfirst, b], W=[dd])
                    op("dve", "tensor_tensor", out=dd[:], in0=dd[:], in1=gt[:], op=ALU.mult, R=[dd, gt], W=[dd])
                    op("dve", "tensor_tensor", out=V_[:, q, :], in0=dd[:], in1=b[:], op=ALU.add, R=[dd, b], W=[V_])
            mix(3)
            stage1(slA, 0, 128, 0, 64, lo64, AF.Tanh)
            for q in range(4):
                c = half * 4 + q
                b = next_ps()
                op("pe", "matmul", b[:], lhsT=slB[:, c * 128:(c + 1) * 128], rhs=lo64[:], start=True, stop=True, R=[slB, lo64], W=[b])
                sgm = F[2 + q % 2]
                op("act", "activation", out=sgm[:], in_=b[:], func=AF.Sigmoid, bias=vcol("w0", rl, c), R=[b, vecs], W=[sgm])
                op("dve", "tensor_scalar", out=LW_[:, q, :], in0=sgm[:], scalar1=NEG_E, scalar2=None, op0=ALU.mult, R=[sgm], W=[LW_])
            mix(4)
            stage1(slA, 1024, 128, 0, 64, lo64, AF.Copy)
            for q in range(4):
                c = half * 4 + q
                b = next_ps()
                op("pe", "matmul", b[:], lhsT=slB[:, 1024 + c * 128:1024 + (c + 1) * 128], rhs=lo64[:], start=True, stop=True, R=[slB, lo64], W=[b])
                op("act", "activation", out=A_[:, q, :], in_=b[:], func=AF.Sigmoid, bias=vcol("a0", rl, c), R=[b, vecs], W=[A_])
            slA2 = load_piece(S_lora, pA2, 2048)
            slC = load_piece(S_lora, pC, 2048)
            mix(5)
            stage1(slA2, 0, 256, 0, 128, lo128, AF.Sigmoid)
            stage1(slA2, 0, 256, 128, 32, lo32, AF.Sigmoid)
            for q in range(4):
                c = half * 4 + q
                b = next_ps()
                op("pe", "matmul", b[:], lhsT=slC[:, c * 128:(c + 1) * 128], rhs=lo128[:], start=True, stop=False, R=[slC, lo128], W=[b])
                op("pe", "matmul", b[:], lhsT=slC[:, 1024 + c * 128:1024 + (c + 1) * 128], rhs=lo32[:], start=False, stop=True, R=[slC, lo32], W=[b])
                op("act", "activation", out=G_[:, q, :], in_=b[:], func=AF.Copy, R=[b], W=[G_])
            for q in range(4):
                if "wkvskip" in C.dbg:
                    op("dve", "tensor_tensor", out=ubuf[:, half * 4 + q, :], in0=G_[:, q, :], in1=V_[:, q, :], op=ALU.mult, R=[G_, V_], W=[ubuf])
                    op("dve", "tensor_tensor", out=ubuf[:, half * 4 + q, :], in0=ubuf[:, half * 4 + q, :], in1=A_[:, q, :], op=ALU.mult, R=[ubuf, A_], W=[ubuf])
                    op("dve", "tensor_tensor", out=ubuf[:, half * 4 + q, :], in0=ubuf[:, half * 4 + q, :], in1=LW_[:, q, :], op=ALU.mult, R=[ubuf, LW_], W=[ubuf])
                    continue
                wkv(rl, half * 4 + q, q)
        out_proj(S_o, rl * 2)

    def wkv(rl, c, q):
        F0, F1, F2, F3, F4 = F
        F5, F6 = Z0, Z1
        f5 = F5[:, 0:T]; f6 = F6[:, 0:T]
        sq_b = xs[:, 0, :]; rk_b = xs[:, 1, :]; y_b = xs[:, 2, :]; d_b = xs[:, 3, :]
        TT = "tensor_tensor"
        op("dve", "tensor_scalar", out=F0[:], in0=K_[:, q, :], scalar1=vcol("k_k", rl, c), scalar2=None, op0=ALU.mult, R=[K_, vecs], W=[F0])
        op("dve", TT, out=sq_b, in0=F0[:], in1=F0[:], op=ALU.mult, R=[F0], W=[xs])
        b = next_ps()
        op("pe", "matmul", b[:], lhsT=onesBD[:], rhs=sq_b, start=True, stop=True, R=[onesBD, xs], W=[b])
        op("act", "activation", out=F1[:], in_=b[:], func=AF.Sqrt, bias=1e-24, R=[b], W=[F1])
        op("dve", "reciprocal", out=F1[:], in_=F1[:], R=[F1], W=[F1])
        op("dve", TT, out=F0[:], in0=F0[:], in1=F1[:], op=ALU.mult, R=[F0, F1], W=[F0])
        op("dve", "tensor_scalar", out=F2[:], in0=A_[:, q, :], scalar1=vcol("k_a", rl, c), scalar2=None, op0=ALU.mult, R=[A_, vecs], W=[F2])
        op("dve", "tensor_scalar", out=F2[:], in0=F2[:], scalar1=kam1[:, rl * 8 + c:rl * 8 + c + 1], scalar2=None, op0=ALU.add, R=[F2, kam1], W=[F2])
        op("dve", TT, out=F2[:], in0=F2[:], in1=K_[:, q, :], op=ALU.mult, R=[F2, K_], W=[F2])
        op("dve", "tensor_tensor_scan", out=F3[:], data0=SM[:], data1=LW_[:, q, :], initial=0.0, op0=ALU.mult, op1=ALU.add, R=[SM, LW_], W=[F3])
        op("act", "activation", out=F4[:], in_=F3[:], func=AF.Exp, R=[F3], W=[F4])
        op("dve", "reciprocal", out=f5, in_=F4[:], R=[F4], W=[F5])
        op("dve", TT, out=f6, in0=F3[:], in1=LW_[:, q, :], op=ALU.subtract, R=[F3, LW_], W=[F6])
        op("act", "activation", out=f6, in_=f6, func=AF.Exp, R=[F6], W=[F6])
        op("dve", "scalar_tensor_tensor", out=rk_b, in0=R_[:, q, :], scalar=vcol("r_k", rl, c), in1=F2[:], op0=ALU.mult, op1=ALU.mult,
           R=[R_, vecs, F2], W=[xs])
        b = next_ps()
        op("pe", "matmul", b[:], lhsT=onesBD[:], rhs=rk_b, start=True, stop=True, R=[onesBD, xs], W=[b])
        op("dve", TT, out=F1[:], in0=V_[:, q, :], in1=b[:], op=ALU.mult, R=[V_, b], W=[F1])
        hv_ = half_view
        for hh in range(2):
            cols = slice(hh * 64, hh * 64 + 64); m_ = mk[:, hh:hh + 1]; nm_ = mk[:, 2 + hh:3 + hh]
            op("dve", "scalar_tensor_tensor", out=BD_ar[:, :, 0, cols], in0=hv_(F0[:, :]), scalar=nm_, in1=hv_(F6[:, 0:T]),
               op0=ALU.mult, op1=ALU.mult, R=[F0, F6, mk], W=[BD_ar])
            op("dve", "scalar_tensor_tensor", out=BD_ar[:, :, 1, cols], in0=hv_(R_[:, q, :]), scalar=m_, in1=hv_(F4[:, :]),
               op0=ALU.mult, op1=ALU.mult, R=[R_, F4, mk], W=[BD_ar])
            op("dve", "scalar_tensor_tensor", out=BD_k[:, :, cols], in0=hv_(F2[:, :]), scalar=m_, in1=hv_(F5[:, 0:T]),
               op0=ALU.mult, op1=ALU.mult, R=[F2, F5, mk], W=[BD_k])
            op("dve", "tensor_scalar", out=BD_v[:, :, cols], in0=hv_(V_[:, q, :]), scalar1=m_, scalar2=None, op0=ALU.mult, R=[V_, mk], W=[BD_v])
        op("dve", TT, out=F0[:], in0=F0[:], in1=A_[:, q, :], op=ALU.mult, R=[F0, A_], W=[F0])
        for hh in range(2):
            cols = slice(hh * 64, hh * 64 + 64); m_ = mk[:, hh:hh + 1]
            op("dve", "scalar_tensor_tensor", out=BD_b[:, :, cols], in0=hv_(F0[:, :]), scalar=m_, in1=hv_(F5[:, 0:T]),
               op0=ALU.mult, op1=ALU.mult, R=[F0, F5, mk], W=[BD_b])
        if "stopPrep" in C.dbg:
            op("dve", TT, out=ubuf[:, c, :], in0=F1[:], in1=G_[:, q, :], op=ALU.mult, R=[F1, G_], W=[ubuf])
            return
        for c8 in range(8):
            ar = BD_ar[:, c8, :, :].rearrange("p a n -> p (a n)")
            b = next_ps()
            for j_, (lh_, lb_) in enumerate(((BD_b, BD_b), (BD_k, BD_k))):
                for a_ in range(2):
                    o0 = j_ * 256 + a_ * 128
                    op("pe", "matmul", b[:, o0:o0 + 128], lhsT=lh_[:, c8, :], rhs=BD_ar[:, c8, a_, :], start=True, stop=True, R=[lb_, BD_ar], W=[b])
            op("dve", TT, out=PQ[c8][:, 0:128], in0=b[:, 0:128], in1=mSU[:], op=ALU.mult, R=[b, mSU], W=[PQ[c8]])
            op("dve", TT, out=AM[c8][:, :], in0=b[:, 128:512], in1=m3[:], op=ALU.mult, R=[b, m3], W=[AM[c8]])
            b2 = next_ps()
            op("pe", "matmul", b2[:, 0:128], lhsT=BD_ar[:, c8, 0, :], rhs=BD_b[:, c8, :], start=True, stop=True, R=[BD_ar, BD_b], W=[b2])
            op("pe", "matmul", b2[:, 128:256], lhsT=BD_b[:, c8, :], rhs=ident_b[:], start=True, stop=True, R=[BD_b, ident_b], W=[b2])
            op("pe", "matmul", b2[:, 256:384], lhsT=BD_k[:, c8, :], rhs=ident_b[:], start=True, stop=True, R=[BD_k, ident_b], W=[b2])
            op("pe", "matmul", b2[:, 384:512], lhsT=BD_v[:, c8, :], rhs=ident_b[:], start=True, stop=True, R=[BD_v, ident_b], W=[b2])
            op("dve", TT, out=PQ[c8][:, 128:256], in0=b2[:, 0:128], in1=mSL[:], op=ALU.mult, R=[b2, mSL], W=[PQ[c8]])
            op("act", "activation", out=TM[c8][:, :], in_=b2[:, 128:512], func=AF.Copy, R=[b2], W=[TM[c8]])
            op("dve", TT, out=XB[c8][:, :], in0=PQ[c8][:, 0:128], in1=ident_b[:], op=ALU.add, R=[PQ[c8], ident_b], W=[XB[c8]])
        if "stopA" in C.dbg:
            op("dve", TT, out=ubuf[:, c, :], in0=F1[:], in1=G_[:, q, :], op=ALU.mult, R=[F1, G_], W=[ubuf])
            return
        for k in range(6):
            for c8 in range(8):
                b = next_ps()
                Pk = PQ[c8][:, 0:128]; Qk = PQ[c8][:, 128:256]
                if k >= 1:
                    op("pe", "matmul", b[:, 256:384], lhsT=Qk, rhs=XB[c8][:, :], start=True, stop=True, R=[PQ[c8], XB[c8]], W=[b])
                if k < 5:
                    if k < 4:
                        op("pe", "matmul", b[:, 0:128], lhsT=Qk, rhs=Pk, start=True, stop=True, R=[PQ[c8]], W=[b])
                    op("pe", "matmul", b[:, 128:256], lhsT=Pk, rhs=Qk, start=True, stop=True, R=[PQ[c8]], W=[b])
                if k >= 1:
                    op("dve", TT, out=XB[c8][:, :], in0=XB[c8][:, :], in1=b[:, 256:384], op=ALU.add, R=[XB[c8], b], W=[XB[c8]])
                if k < 4:
                    op("act", "activation", out=PQ[c8][:, :], in_=b[:, 0:256], func=AF.Copy, R=[b], W=[PQ[c8]])
                elif k == 4:
                    op("act", "activation", out=PQ[c8][:, 128:256], in_=b[:, 128:256], func=AF.Copy, R=[b], W=[PQ[c8]])
        if "stopInv" in C.dbg:
            op("dve", TT, out=ubuf[:, c, :], in0=F1[:], in1=G_[:, q, :], op=ALU.mult, R=[F1, G_], W=[ubuf])
            return
        Sf = S_f[rl][c]; Sb = S_b[rl][c]
        for c8 in range(8):
            rb = rhsb[c8 % 2]; sa = sab[c8 % 2]
            at = BD_ar[:, c8, 0, :]; rt = BD_ar[:, c8, 1, :]
            ARB = AM[c8][:, 0:128]; AAK = AM[c8][:, 128:256]; ARK = AM[c8][:, 256:384]
            Btm = TM[c8][:, 0:128]; Ktm = TM[c8][:, 128:256]; Vtm = TM[c8][:, 256:384]
            b = next_ps()
            op("pe", "matmul", b[:, 0:128], lhsT=at, rhs=Sb[:], start=True, stop=False, R=[BD_ar, Sb], W=[b])
            op("pe", "matmul", b[:, 0:128], lhsT=AAK, rhs=Vtm, start=False, stop=True, R=[AM[c8], TM[c8]], W=[b])
            op("act", "activation", out=rb[:], in_=b[:, 0:128], func=AF.Copy, R=[b], W=[rb])
            b = next_ps()
            op("pe", "matmul", b[:, 0:128], lhsT=XB[c8][:, :], rhs=rb[:], start=True, stop=True, R=[XB[c8], rb], W=[b])
            op("act", "activation", out=sa[:], in_=b[:, 0:128], func=AF.Copy, R=[b], W=[sa])
            by = next_ps()
            op("pe", "matmul", by[:, 0:128], lhsT=Sb[:], rhs=rt, start=True, stop=False, R=[Sb, BD_ar], W=[by])
            op("pe", "matmul", by[:, 0:128], lhsT=sa[:], rhs=ARB, start=False, stop=False, R=[sa, AM[c8]], W=[by])
            op("pe", "matmul", by[:, 0:128], lhsT=Vtm, rhs=ARK, start=False, stop=True, R=[TM[c8], AM[c8]], W=[by])
            yc = F3[:, c8 * 64:(c8 + 1) * 64]
            op("act", "activation", out=yc, in_=by[:, 0:64], func=AF.Copy, R=[by], W=[F3])
            op("dve", TT, out=yc, in0=yc, in1=by[:, 64:128], op=ALU.add, R=[F3, by], W=[F3])
            bs = next_ps()
            op("pe", "matmul", bs[:, 0:128], lhsT=Btm, rhs=sa[:], start=True, stop=False, R=[TM[c8], sa], W=[bs])
            op("pe", "matmul", bs[:, 0:128], lhsT=Ktm, rhs=Vtm, start=False, stop=True, R=[TM[c8]], W=[bs])
            op("dve", TT, out=Sf[:], in0=Sf[:], in1=bs[:, 0:128], op=ALU.add, R=[Sf, bs], W=[Sf])
            op("dve", "tensor_scalar", out=Sf[:], in0=Sf[:], scalar1=F4[:, c8 * 64 + 63:c8 * 64 + 64], scalar2=None, op0=ALU.mult, R=[Sf, F4], W=[Sf])
            op("act", "activation", out=Sb[:], in_=Sf[:], func=AF.Copy, R=[Sf], W=[Sb])
        if "stopSeq" in C.dbg:
            op("dve", TT, out=ubuf[:, c, :], in0=F1[:], in1=G_[:, q, :], op=ALU.mult, R=[F1, G_], W=[ubuf])
            return
        op("act", "activation", out=y_b, in_=F3[:], func=AF.Copy, R=[F3], W=[xs])
        b = next_ps()
        op("pe", "matmul", b[:], lhsT=onesBD[:], rhs=y_b, start=True, stop=True, R=[onesBD, xs], W=[b])
        op("dve", "scalar_tensor_tensor", out=f5, in0=b[:], scalar=-1.0 / 64, in1=F3[:], op0=ALU.mult, op1=ALU.add, R=[b, F3], W=[F5])
        op("dve", TT, out=d_b, in0=f5, in1=f5, op=ALU.mult, R=[F5], W=[xs])
        b = next_ps()
        op("pe", "matmul", b[:], lhsT=onesBD[:], rhs=d_b, start=True, stop=True, R=[onesBD, xs], W=[b])
        op("act", "activation", out=f6, in_=b[:], func=AF.Sqrt, bias=64e-5, scale=1.0 / 64, R=[b], W=[F6])
        op("dve", "reciprocal", out=f6, in_=f6, R=[F6], W=[F6])
        op("dve", TT, out=f5, in0=f5, in1=f6, op=ALU.mult, R=[F5, F6], W=[F5])
        op("dve", "tensor_scalar", out=f5, in0=f5, scalar1=vcol("ln_w", rl, c), scalar2=vcol("ln_b", rl, c), op0=ALU.mult, op1=ALU.add,
           R=[F5, vecs], W=[F5])
        op("dve", TT, out=f5, in0=f5, in1=F1[:], op=ALU.add, R=[F5, F1], W=[F5])
        op("dve", TT, out=ubuf[:, c, :], in0=f5, in1=G_[:, q, :], op=ALU.mult, R=[F5, G_], W=[ubuf])

    return rwkv


def pack_vecs(inp):
    def cols(v):
        return np.asarray(v, np.float32).reshape(-1, D)
    parts = {"norm_g": cols(inp["norm_g"]), "final_g": cols(inp["final_g"]), "conv_w": cols(inp["conv_w"]),
             "mu": cols(inp["rwkv_mu"]), "w0": cols(inp["rwkv_w0"]), "a0": cols(inp["rwkv_a0"]), "k_k": cols(inp["rwkv_k_k"]),
             "k_a": cols(inp["rwkv_k_a"]), "r_k": cols(inp["rwkv_r_k"]), "ln_w": cols(inp["rwkv_ln_w"]), "ln_b": cols(inp["rwkv_ln_b"]),
             "v0": cols(inp["rwkv_v0"])}
    out = np.zeros((128, NVEC), np.float32)
    for n, cnt in VEC_LAYOUT:
        a = parts[n]
        assert a.shape[0] == cnt, (n, a.shape)
        out[:, VOFF[n]:VOFF[n] + cnt * NCH] = a.reshape(cnt, NCH, 128).transpose(2, 0, 1).reshape(128, cnt * NCH)
    return out


def make_in_maps(inp, ncores=8):
    f = lambda k, shape: np.ascontiguousarray(np.asarray(inp[k], np.float32).reshape(shape))
    shared = {
        "vecs_in": pack_vecs(inp),
        "ffn_w_gu": f("ffn_w_gu", (8, D, 2 * FF)), "ffn_w_d": f("ffn_w_d", (8, FF, D)),
        "conv_w_in": f("conv_w_in", (2, D, 3 * D)), "conv_w_out": f("conv_w_out", (2, D, D)),
        "rwkv_w_rkv": f("rwkv_w_rkv", (6, D, D)), "rwkv_w_o": f("rwkv_w_o", (2, D, D)),
        "rwkv_w1": f("rwkv_w1", (2, D, 64)), "rwkv_w2": f("rwkv_w2", (2, 64, D)),
        "rwkv_a1": f("rwkv_a1", (2, D, 64)), "rwkv_a2": f("rwkv_a2", (2, 64, D)),
        "rwkv_g1": f("rwkv_g1", (2, D, 160)), "rwkv_g2": f("rwkv_g2", (2, 160, D)),
        "rwkv_v1": f("rwkv_v1", (1, D, 32)), "rwkv_v2": f("rwkv_v2", (1, 32, D)),
    }
    x = np.asarray(inp["x"], np.float32)
    maps = []
    for c in range(ncores):
        m = dict(shared); m["x"] = np.ascontiguousarray(x[c]); maps.append(m)
    return maps


def kernel(**inputs):
    nc = build()
    maps = make_in_maps(inputs)
    res = run_bass_kernel_spmd(nc, maps, core_ids=list(range(8)))
    return np.stack([np.asarray(r["out"], np.float32) for r in res.results], axis=0)
```
